# Optimizing a Trainium2 kernel written in Bass

```python
import functools
import jax, jax.numpy as jnp
from jax import lax
import numpy as np

D_MODEL = 4096
BATCH = 1
SEQ = 8192
DEPTH = 1
DEC_BATCH = 32
DEC_SEQ = 8
PAST_LEN = 8192
PAGE_SIZE = 128

DIL_GROUPS = ((128, 1), (512, 4), (2048, 16))
N_GROUPS = 3
A_HEADS = D_MODEL // 512
HEAD_DIM = 128
A_WIDTH = A_HEADS * HEAD_DIM
BAND_BLOCK = 128
HG_HEADS = D_MODEL // 256
HG_KDIM = 128
HG_VDIM = 128
HG_FDIM = HG_HEADS * HG_KDIM
HG_WIDTH = HG_HEADS * HG_VDIM
HG_CHUNK = 32
FFN_HIDDEN = 4 * D_MODEL
PLE_DIM = 256
IN_SIZES = (N_GROUPS * A_WIDTH, N_GROUPS * A_WIDTH, N_GROUPS * A_WIDTH, HG_FDIM, HG_FDIM, HG_WIDTH, HG_WIDTH, D_MODEL, D_MODEL)
IN_WIDTH = sum(IN_SIZES)
NORM_EPS = 1e-6

kernel_name = 'hybrid_dilated_attn_hgrn2_decoder_step'


def rms_norm(x, g):
    xf = x.astype(jnp.float32)
    y = xf * lax.rsqrt(jnp.mean(xf * xf, axis=-1, keepdims=True) + NORM_EPS)
    return (y * g.astype(jnp.float32)).astype(x.dtype)


def _alibi_slopes():
    n = N_GROUPS * A_HEADS
    e = jnp.arange(1, n + 1, dtype=jnp.float32)
    return jnp.exp2(-8.0 * e / n).reshape(N_GROUPS, A_HEADS)


def _split_cols(z):
    out, start = [], 0
    for s in IN_SIZES:
        out.append(z[..., start:start + s])
        start += s
    return out


def _band_dilated_attention(q, k, v, window, dil, slopes):
    B, S, H, D = q.shape
    nk = window // dil
    L = S // dil
    nb = -(-L // BAND_BLOCK)
    Lp = nb * BAND_BLOCK

    def split(a):
        a = a.reshape(B, L, dil, H, D).transpose(0, 2, 1, 3, 4)
        a = jnp.pad(a, ((0, 0), (0, 0), (0, Lp - L), (0, 0), (0, 0)))
        return a.reshape(B, dil, nb, BAND_BLOCK, H, D)

    def with_prev(a):
        prev = jnp.pad(a, ((0, 0), (0, 0), (1, 0), (0, 0), (0, 0), (0, 0)))[:, :, :-1]
        return jnp.concatenate([prev, a], axis=3)

    qb = split(q)
    kk, vv = with_prev(split(k)), with_prev(split(v))
    logits = jnp.einsum('bznihd,bznjhd->bznhij', qb, kk).astype(jnp.float32) * (HEAD_DIM ** -0.5)
    i = jnp.arange(BAND_BLOCK)[:, None]
    j = jnp.arange(2 * BAND_BLOCK)[None, :]
    delta = BAND_BLOCK + i - j
    key_u = (jnp.arange(nb)[:, None, None] - 1) * BAND_BLOCK + j[None]
    valid = (delta >= 0)[None] & (delta <= nk)[None] & (key_u >= 0)
    bias = -slopes[:, None, None] * (delta * dil).astype(jnp.float32)[None]
    logits = jnp.where(valid[:, None], logits + bias, -jnp.inf)
    lse = jax.nn.logsumexp(logits, axis=-1)
    probs = jnp.exp(logits - lse[..., None])
    o = jnp.einsum('bznhij,bznjhd->bznihd', probs, vv.astype(jnp.float32))
    o = o.reshape(B, dil, Lp, H, D)[:, :, :L].transpose(0, 2, 1, 3, 4).reshape(B, S, H, D)
    lse = lse.transpose(0, 1, 2, 4, 3).reshape(B, dil, Lp, H)[:, :, :L].transpose(0, 2, 1, 3).reshape(B, S, H)
    return o, lse


def _merge_dilations(outs, lses, dtype):
    w = jax.nn.softmax(jnp.stack(lses, axis=0), axis=0)
    o = jnp.sum(w[..., None] * jnp.stack(outs, axis=0), axis=0)
    return o.reshape(o.shape[0], o.shape[1], A_WIDTH).astype(dtype)


def _attend_prompt(q, k, v, slopes):
    S = q.shape[1]
    outs, lses, rows = [], [], []
    for g, (window, dil) in enumerate(DIL_GROUPS):
        o, lse = _band_dilated_attention(q[:, :, g], k[:, :, g], v[:, :, g], window, dil, slopes[g])
        outs.append(o)
        lses.append(lse)
        L = min(window, S)
        rows.append(jnp.stack([k[:, S - L:, g], v[:, S - L:, g]], axis=2))
    return _merge_dilations(outs, lses, q.dtype), rows


def _attend_sample(q, k, v, caches, slopes):
    T = q.shape[1]
    outs, lses, rows = [], [], []
    for g, (window, dil) in enumerate(DIL_GROUPS):
        buf = caches[g]
        L = buf.shape[1]
        k_all = jnp.concatenate([buf[:, :, 0], k[:, :, g]], axis=1)
        v_all = jnp.concatenate([buf[:, :, 1], v[:, :, g]], axis=1)
        jj = jnp.arange(window // dil + 1)
        idx = (L + jnp.arange(T))[:, None] - dil * jj[None, :]
        valid = idx >= 0
        idx = jnp.maximum(idx, 0)
        ks, vs = k_all[:, idx], v_all[:, idx]
        logits = jnp.einsum('bthd,btjhd->bthj', q[:, :, g], ks).astype(jnp.float32) * (HEAD_DIM ** -0.5)
        bias = -slopes[g][:, None] * (dil * jj).astype(jnp.float32)[None, :]
        logits = jnp.where(valid[None, :, None, :], logits + bias, -jnp.inf)
        lse = jax.nn.logsumexp(logits, axis=-1)
        probs = jnp.exp(logits - lse[..., None])
        outs.append(jnp.einsum('bthj,btjhd->bthd', probs, vs.astype(jnp.float32)))
        lses.append(lse)
        rows.append(jnp.stack([k[:, :, g], v[:, :, g]], axis=2))
    return _merge_dilations(outs, lses, q.dtype), rows


def _gla_chunked(q, k, v, log_g, s0):
    B, T, H, K = q.shape
    V = v.shape[-1]
    n = -(-T // HG_CHUNK)
    pad = n * HG_CHUNK - T

    def prep(a):
        a = jnp.pad(a, ((0, 0), (0, pad), (0, 0), (0, 0)))
        return a.reshape(B, n, HG_CHUNK, H, a.shape[-1]).transpose(1, 0, 2, 3, 4)

    tri = jnp.tril(jnp.ones((HG_CHUNK, HG_CHUNK), dtype=bool))

    def step(S, inp):
        qc, kc, vc, gc = inp
        b = jnp.cumsum(gc, axis=1)
        q_dec = qc * jnp.exp(b)
        k_dec = kc * jnp.exp(-b)
        o = jnp.einsum('bchk,bhkv->bchv', q_dec, S)
        A = jnp.where(tri, jnp.einsum('bthk,bshk->bhts', q_dec, k_dec), 0.0)
        o = o + jnp.einsum('bhts,bshv->bthv', A, vc)
        b_last = b[:, -1]
        S = S * jnp.exp(b_last)[..., None] + jnp.einsum('bshk,bshv->bhkv', kc * jnp.exp(b_last[:, None] - b), vc)
        return S, o

    S, o = lax.scan(step, s0, (prep(q), prep(k), prep(v), prep(log_g)))
    o = o.transpose(1, 0, 2, 3, 4).reshape(B, n * HG_CHUNK, H, V)[:, :T]
    return o, S


def _hgrn2_branch(hq, hf, hi, hog, lb, g_out, s0):
    B, T, _ = hq.shape
    f32 = jnp.float32
    q = jax.nn.silu(hq.astype(f32)).reshape(B, T, HG_HEADS, HG_KDIM) * (HG_KDIM ** -0.5)
    f = hf.astype(f32)
    gate = lb + (1.0 - lb) * jax.nn.sigmoid(f)
    log_g = jnp.log(gate).reshape(B, T, HG_HEADS, HG_KDIM)
    k = ((1.0 - lb) * jax.nn.sigmoid(-f)).reshape(B, T, HG_HEADS, HG_KDIM)
    v = hi.astype(f32).reshape(B, T, HG_HEADS, HG_VDIM)
    o, s_new = _gla_chunked(q, k, v, log_g, s0.astype(f32))
    o = rms_norm(o, g_out) * jax.nn.sigmoid(hog.astype(f32).reshape(B, T, HG_HEADS, HG_VDIM))
    return o.reshape(B, T, HG_WIDTH).astype(hq.dtype), s_new.astype(s0.dtype)


def _layer(x, p, s0, attend, lb, g_mix, w_in, g_q, g_k, g_hg_out, w_up_attn, w_up_hgrn, w_out,
           g_ffn, w_ff_up, w_ff_down, w_ple, w_ple_gate):
    B, T, _ = x.shape
    n = rms_norm(x, g_mix)
    qa, ka, va, hq, hf, hi, hog, ga, gb = _split_cols(n @ w_in)
    qa = rms_norm(qa.reshape(B, T, N_GROUPS, A_HEADS, HEAD_DIM), g_q[:, None, :])
    ka = rms_norm(ka.reshape(B, T, N_GROUPS, A_HEADS, HEAD_DIM), g_k[:, None, :])
    va = va.reshape(B, T, N_GROUPS, A_HEADS, HEAD_DIM)
    o_attn, kv_rows = attend(qa, ka, va)
    o_hg, s_new = _hgrn2_branch(hq, hf, hi, hog, lb, g_hg_out, s0)
    merged = jax.nn.sigmoid(ga) * (o_attn @ w_up_attn) + jax.nn.sigmoid(gb) * (o_hg @ w_up_hgrn)
    x = x + merged @ w_out
    h = rms_norm(x, g_ffn)
    x = x + jnp.square(jax.nn.relu(h @ w_ff_up)) @ w_ff_down
    x = x + jax.nn.sigmoid(x @ w_ple_gate) * (p @ w_ple)
    return x, kv_rows, s_new


def setup_inputs(seed: int = 0) -> dict:
    key = jax.random.key(seed)
    ks = jax.random.split(key, 24)
    f32 = jnp.float32

    def nrm(k, shape, scale=1.0):
        return jax.random.normal(k, shape, f32) * scale

    def gain(k, shape):
        return 1.0 + 0.02 * jax.random.normal(k, shape, f32)

    buf_len = [min(w, PAST_LEN) for w, _ in DIL_GROUPS]
    return {
        'x_prompt': nrm(ks[0], (BATCH, SEQ, D_MODEL)),
        'x_sample': nrm(ks[1], (DEC_BATCH, DEC_SEQ, D_MODEL)),
        'cache_kv_w128': nrm(ks[2], (DEPTH, DEC_BATCH, buf_len[0], 2, A_HEADS, HEAD_DIM)),
        'cache_kv_w512': nrm(ks[3], (DEPTH, DEC_BATCH, buf_len[1], 2, A_HEADS, HEAD_DIM)),
        'cache_kv_w2048': nrm(ks[4], (DEPTH, DEC_BATCH, buf_len[2], 2, A_HEADS, HEAD_DIM)),
        'state_hgrn': nrm(ks[5], (DEPTH, DEC_BATCH, HG_HEADS, HG_KDIM, HG_VDIM), 0.5),
        'p_prompt': nrm(ks[6], (DEPTH, BATCH, SEQ, PLE_DIM)),
        'p_sample': nrm(ks[7], (DEPTH, DEC_BATCH, DEC_SEQ, PLE_DIM)),
        'g_mix': gain(ks[8], (DEPTH, D_MODEL)),
        'w_in': nrm(ks[9], (DEPTH, D_MODEL, IN_WIDTH), D_MODEL ** -0.5),
        'g_q': gain(ks[10], (DEPTH, N_GROUPS, HEAD_DIM)),
        'g_k': gain(ks[11], (DEPTH, N_GROUPS, HEAD_DIM)),
        'hg_lb_raw': nrm(ks[12], (DEPTH + 1, HG_FDIM), 0.1),
        'g_hg_out': gain(ks[13], (DEPTH, HG_VDIM)),
        'w_up_attn': nrm(ks[14], (DEPTH, A_WIDTH, D_MODEL), A_WIDTH ** -0.5),
        'w_up_hgrn': nrm(ks[15], (DEPTH, HG_WIDTH, D_MODEL), HG_WIDTH ** -0.5),
        'w_out': nrm(ks[16], (DEPTH, D_MODEL, D_MODEL), D_MODEL ** -0.5),
        'g_ffn': gain(ks[17], (DEPTH, D_MODEL)),
        'w_ff_up': nrm(ks[18], (DEPTH, D_MODEL, FFN_HIDDEN), D_MODEL ** -0.5),
        'w_ff_down': nrm(ks[19], (DEPTH, FFN_HIDDEN, D_MODEL), FFN_HIDDEN ** -0.5),
        'w_ple': nrm(ks[20], (DEPTH, PLE_DIM, D_MODEL), PLE_DIM ** -0.5),
        'w_ple_gate': nrm(ks[21], (DEPTH, D_MODEL, D_MODEL), D_MODEL ** -0.5),
    }


def reference(x_prompt, x_sample, cache_kv_w128, cache_kv_w512, cache_kv_w2048, state_hgrn,
              p_prompt, p_sample, g_mix, w_in, g_q, g_k, hg_lb_raw, g_hg_out, w_up_attn, w_up_hgrn,
              w_out, g_ffn, w_ff_up, w_ff_down, w_ple, w_ple_gate):
    slopes = _alibi_slopes()
    lb_all = jnp.cumsum(jax.nn.softmax(hg_lb_raw.astype(jnp.float32), axis=0), axis=0)
    yp, ys = x_prompt, x_sample
    kvp = ([], [], [])
    kvs = ([], [], [])
    stp, sts = [], []
    for i in range(DEPTH):
        lw = (lb_all[i], g_mix[i], w_in[i], g_q[i], g_k[i], g_hg_out[i], w_up_attn[i], w_up_hgrn[i],
              w_out[i], g_ffn[i], w_ff_up[i], w_ff_down[i], w_ple[i], w_ple_gate[i])
        s0p = jnp.zeros((x_prompt.shape[0], HG_HEADS, HG_KDIM, HG_VDIM), x_prompt.dtype)
        yp, rows_p, s_p = _layer(yp, p_prompt[i], s0p, functools.partial(_attend_prompt, slopes=slopes), *lw)
        caches = (cache_kv_w128[i], cache_kv_w512[i], cache_kv_w2048[i])
        ys, rows_s, s_s = _layer(ys, p_sample[i], state_hgrn[i],
                                 functools.partial(_attend_sample, caches=caches, slopes=slopes), *lw)
        for g in range(N_GROUPS):
            kvp[g].append(rows_p[g])
            kvs[g].append(rows_s[g])
        stp.append(s_p)
        sts.append(s_s)
    kv128_p, kv512_p, kv2048_p = jnp.stack(kvp[0]), jnp.stack(kvp[1]), jnp.stack(kvp[2])
    kv128_s, kv512_s, kv2048_s = jnp.stack(kvs[0]), jnp.stack(kvs[1]), jnp.stack(kvs[2])
    hg_p, hg_s = jnp.stack(stp), jnp.stack(sts)
    return (yp, ys, kv128_p, kv512_p, kv2048_p, hg_p, kv128_s, kv512_s, kv2048_s, hg_s)
```

```python
import contextlib
import os
import numpy as np
import concourse.bass as bass
import concourse.mybir as mybir
from concourse.bass_utils import run_bass_kernel_spmd

F32 = mybir.dt.float32
BF16 = mybir.dt.bfloat16
I32 = mybir.dt.int32
ALU = mybir.AluOpType
AF = mybir.ActivationFunctionType
AX = mybir.AxisListType

NC_ = 8
D = 4096
TL = 1056
NTOK = NC_ * TL
NMINE = 2176
GROUPS = ((128, 1), (512, 4), (2048, 16))
NKT = (2, 5, 17)
KOFF = (0, 2, 7)
CT = (1, 4, 16)
COFF = (0, 1, 5)
EPS = 1e-6
NEG = -30000.0
DBG = os.environ.get("KDBG", "")
KNT = int(os.environ.get("KNT", "64"))
KPARTS = os.environ.get("KPARTS", "abcd")
KSKIP12 = bool(os.environ.get("KSKIP12", ""))


class Buf:
    __slots__ = ("name", "w", "r", "excl", "wtok")

    def __init__(self, name, excl=False):
        self.name = name
        self.w = []
        self.r = []
        self.wtok = None
        self.excl = excl


class Sched:
    NDMA = 6

    def __init__(self, nc, stack):
        self.nc = nc
        self.eng = {"pe": nc.tensor, "act": nc.scalar, "dve": nc.vector, "pool": nc.gpsimd, "sp": nc.sync}
        self.sems = {}
        self.cnt = {}
        for e in self.eng:
            self.sems[e] = stack.enter_context(nc.semaphore("c_" + e))
            self.cnt[e] = 0
        self.dq = {}
        for q in ("sp", "pool"):
            lst = []
            for i in range(self.NDMA):
                k = "d_%s%d" % (q, i)
                self.sems[k] = stack.enter_context(nc.semaphore(k))
                self.cnt[k] = 0
                lst.append(k)
            self.dq[q] = [lst, 0]
        self.sems["cc"] = stack.enter_context(nc.semaphore("ccsem"))
        self.cnt["cc"] = 0
        self.seen = {e: {} for e in self.eng}
        self.ninst = 0
        self.log = {e: [] for e in self.eng}

    def _wait(self, e, deps):
        best = {}
        for d in deps:
            if d is None:
                continue
            k, v = d
            if best.get(k, 0) < v:
                best[k] = v
        for k, v in best.items():
            if e == "pe" and k == "pe":
                continue
            if self.seen[e].get(k, 0) < v:
                self.eng[e].wait_ge(self.sems[k], v)
                self.log[e].append(("w", k, v))
                self.seen[e][k] = v

    @staticmethod
    def _deps(reads, writes, tok=None):
        deps = []
        for b in reads:
            deps.extend(b.w)
            if b.excl:
                deps.extend(b.r)
        for b in writes:
            if tok is None or b.wtok != tok:
                deps.extend(b.w)
            deps.extend(b.r)
        return deps

    @staticmethod
    def _commit(ev, reads, writes, tok=None):
        for b in reads:
            b.r.append(ev)
            if len(b.r) > 16:
                best = {}
                for k, v in b.r:
                    if best.get(k, 0) < v:
                        best[k] = v
                b.r = list(best.items())
        for b in writes:
            if tok is not None and b.wtok == tok:
                b.w.append(ev)
            else:
                b.w = [ev]
            b.wtok = tok
            b.r = []

    def op(self, e, fn, reads=(), writes=(), signal=True):
        self._wait(e, self._deps(reads, writes))
        inst = fn()
        self.ninst += 1
        if signal:
            self.cnt[e] += 1
            inst.then_inc(self.sems[e], 1)
            ev = (e, self.cnt[e])
            self.log[e].append(("i", e, 1))
        else:
            ev = (e, self.cnt[e] + 1)
            self.log[e].append(("i", None, 0))
        self._commit(ev, reads, writes)
        return inst

    def dma(self, q, fn, reads=(), writes=(), tok=None):
        lst, i = self.dq[q]
        k = lst[i % self.NDMA]
        self.dq[q][1] = i + 1
        deps = self._deps(reads, writes, tok)
        deps.append((k, self.cnt[k]))
        self._wait(q, deps)
        inst = fn()
        self.ninst += 1
        self.cnt[k] += 16
        inst.then_inc(self.sems[k], 16)
        self.log[q].append(("i", k, 16))
        self._commit((k, self.cnt[k]), reads, writes, tok)
        return inst

    def collective(self, fn, reads=(), writes=()):
        self._wait("pool", self._deps(reads, writes))
        inst = fn()
        self.cnt["cc"] += 1
        inst.then_inc(self.sems["cc"])
        self.log["pool"].append(("i", "cc", 1))
        self._commit(("cc", self.cnt["cc"]), reads, writes)
        return inst

    def barrier(self):
        allv = [(k, v) for k, v in self.cnt.items() if v > 0]
        for e in self.eng:
            self._wait(e, allv)


class _Stop(Exception):
    pass


def build():
    nc = bass.Bass("TRN2", target_bir_lowering=False)
    STOP = int(DBG) if DBG else 99

    in_names = []
    nc._k_in_names = in_names

    def din(name, shape, dt=F32):
        in_names.append(name)
        return nc.dram_tensor(name, list(shape), dt, kind="ExternalInput").ap()

    def dout(name, shape, dt=F32):
        return nc.dram_tensor(name, list(shape), dt, kind="ExternalOutput").ap()

    def dscr(name, shape, dt):
        return nc.dram_tensor(name, list(shape), dt).ap()

    x_own = din("x_own", [TL, D]); p_own = din("p_own", [TL, 256])
    gmix = din("gmix", [1, D]); gffn = din("gffn", [1, D])
    w_mine = din("w_mine", [D, NMINE])
    gq = din("gq", [1, 384]); gk = din("gk", [1, 384]); ghg = din("ghg", [1, 128]); lbraw = din("lbraw", [2, 256])
    bias_p = din("bias_p", [128, 24 * 128]); bias_new = din("bias_new", [128, 384]); bias_c = din("bias_c", [128, 21 * 8])
    ck = [din("ck0", [32, 128, 256]), din("ck1", [32, 512, 256]), din("ck2", [32, 2048, 256])]
    s0 = din("s0", [32, 2, 128, 128])
    ident_d = din("ident", [128, 128])
    hmat_d = din("hmat", [128, 6 * 128])
    seqmask_d = din("seqmask", [128, 16 * 128]); rowmask_d = din("rowmask", [128, 16])
    gidx_d = din("gidx", [128, 24], I32)

    y_o = dout("y", [TL, D]); kvp_o = dout("kvp", [2688, 256]); hgp_o = dout("hgp", [2, 128, 128])
    kvs_o = dout("kvs", [3, 256, 256]); hgs_o = dout("hgs", [32, 2, 128, 128])

    nT_b = dscr("nT_b", [D, TL], BF16); G1 = dscr("G1", [NC_ * D, TL], BF16)
    proj = dscr("proj", [NTOK, NMINE], F32)
    ST = dscr("ST", [NC_ * 384, TL], BF16); G2 = dscr("G2", [NC_ * NC_ * 384, TL], BF16)
    mT = dscr("mT", [D, TL], BF16); xacc = dscr("xacc", [TL, D], F32); hT = dscr("hT", [D, TL], BF16)
    x2T = dscr("x2T", [D, TL], BF16); pT = dscr("pT", [256, TL], BF16)
    b_nT = Buf("nT_b"); b_G1 = Buf("G1"); b_proj = Buf("proj"); b_ST = Buf("ST"); b_G2 = Buf("G2")
    b_mT = Buf("mT"); b_xacc = Buf("xacc"); b_hT = Buf("hT"); b_x2T = Buf("x2T"); b_pT = Buf("pT")
    b_out = Buf("outs")
    dbg_out = dout("dbg", [128, 4096]) if DBG else None

    with contextlib.ExitStack() as st0:
        S = Sched(nc, st0)
        V = nc.vector; A = nc.scalar; PE = nc.tensor; G = nc.gpsimd; SP = nc.sync

        uniq = [0]

        def sb(stk, name, shape, dt):
            uniq[0] += 1
            return stk.enter_context(nc.sbuf_tensor("s%d_%s" % (uniq[0], name), list(shape), dt)), Buf(name)

        PF = st0.enter_context(nc.psum_tensor("PF", [128, 6, 512], F32)); pf = [Buf("pf%d" % i, True) for i in range(6)]
        PB = st0.enter_context(nc.psum_tensor("PB", [128, 2, 1024], BF16)); pb = [Buf("pb%d" % i, True) for i in range(2)]
        identf, b_identf = sb(st0, "identf", [128, 128], F32)
        identb, b_identb = sb(st0, "identb", [128, 128], BF16)
        S.dma("sp", lambda: SP.dma_start(out=identf[:], in_=ident_d), writes=[b_identf])
        S.op("dve", lambda: V.tensor_copy(out=identb[:], in_=identf[:]), reads=[b_identf], writes=[b_identb])
        flip = [0]

        def evac(fn_act, fn_dve, reads, writes):
            flip[0] ^= 1
            if flip[0]:
                return S.op("act", fn_act, reads=reads, writes=writes)
            return S.op("dve", fn_dve, reads=reads, writes=writes)

        def copy_any(out, in_, reads, writes):
            return evac(lambda: A.copy(out=out, in_=in_), lambda: V.tensor_copy(out=out, in_=in_), reads, writes)

        def norm_T(stk, src, b_src, T, Fdim, gvec, dst, b_dst, tag):
            kc_n = Fdim // 128
            xt, b_xt = sb(stk, "xt" + tag, [128, Fdim], F32)
            xn, b_xn = sb(stk, "xn" + tag, [128, Fdim], BF16)
            stg, b_stg = sb(stk, "stg" + tag, [128, kc_n, 128], BF16)
            ss, b_ss = sb(stk, "ss" + tag, [128, 1], F32)
            if gvec is not None:
                gbc, b_gbc = sb(stk, "gbc" + tag, [128, Fdim], F32)
                S.dma("sp", lambda: SP.dma_start(out=gbc[:], in_=gvec.partition_broadcast(128)), writes=[b_gbc])
            nt = (T + 127) // 128
            for tt in range(nt):
                t0 = tt * 128; rows = min(128, T - t0)
                S.dma("sp", lambda: SP.dma_start(out=xt[:rows, :], in_=src[t0:t0 + rows, :]), reads=[b_src], writes=[b_xt])
                if gvec is not None:
                    S.op("act", lambda: A.activation(out=xn[:rows, :], in_=xt[:rows, :], func=AF.Square, accum_out=ss[:rows, :]),
                         reads=[b_xt], writes=[b_xn, b_ss])
                    S.op("dve", lambda: V.tensor_scalar(out=ss[:rows, :], in0=ss[:rows, :], scalar1=1.0 / Fdim, scalar2=EPS,
                                                        op0=ALU.mult, op1=ALU.add), reads=[b_ss], writes=[b_ss])
                    S.op("act", lambda: A.activation(out=ss[:rows, :], in_=ss[:rows, :], func=AF.Sqrt), reads=[b_ss], writes=[b_ss])
                    S.op("dve", lambda: V.reciprocal(out=ss[:rows, :], in_=ss[:rows, :]), reads=[b_ss], writes=[b_ss])
                    S.op("dve", lambda: V.scalar_tensor_tensor(out=xn[:rows, :], in0=xt[:rows, :], scalar=ss[:rows, 0:1], in1=gbc[:rows, :],
                                                               op0=ALU.mult, op1=ALU.mult), reads=[b_xt, b_ss, b_gbc], writes=[b_xn])
                else:
                    S.op("dve", lambda: V.tensor_copy(out=xn[:rows, :], in_=xt[:rows, :]), reads=[b_xt], writes=[b_xn])
                for k0 in range(0, kc_n, 8):
                    nk = min(8, kc_n - k0); bk = (k0 // 8) % 2
                    for j in range(nk):
                        S.op("pe", lambda: PE.transpose(out=PB[:, bk, j * 128:j * 128 + rows], in_=xn[:rows, (k0 + j) * 128:(k0 + j + 1) * 128],
                                                        identity=identb[:rows, :rows]),
                             reads=[b_xn, b_identb], writes=[pb[bk]], signal=(j == nk - 1))
                    src_ps = PB[:, bk, 0:nk * 128].rearrange("p (a b) -> p a b", a=nk)[:, :, 0:rows]
                    copy_any(stg[:, k0:k0 + nk, 0:rows], src_ps, [pb[bk]], [b_stg])
                S.dma("sp", lambda: SP.dma_start(out=dst.rearrange("(kc p) t -> p kc t", p=128)[:, :, t0:t0 + rows], in_=stg[:, :, 0:rows]),
                      reads=[b_stg], writes=[b_dst])

        gemm_ctr = [0]
        slab_ctr = [0]

        def gemm(mode, groups, T, N, cb, wslab, b_wslab, NS):
            koff = []
            o = 0
            for (_, _, kc, _) in groups:
                koff.append(o); o += kc
            for n0 in range(0, N, NS):
                ns = min(NS, N - n0)
                wi = slab_ctr[0] % 2; slab_ctr[0] += 1
                wt = wslab[wi]; bw = b_wslab[wi]
                tok = ("slab", slab_ctr[0])
                for gi, (_, _, kc, W) in enumerate(groups):
                    for kk in range(0, kc, 16):
                        k2 = min(kc, kk + 16)
                        S.dma("pool", lambda: G.dma_start(out=wt[:, koff[gi] + kk:koff[gi] + k2, 0:ns],
                                                          in_=W[kk * 128:k2 * 128, n0:n0 + ns].rearrange("(kc p) n -> p kc n", p=128)),
                              writes=[bw], tok=tok)
                if mode == "tm":
                    for t0 in range(0, T, 128):
                        rows = min(128, T - t0)
                        aps = []; bufs = []
                        for gi, (act, b_act, kc, W) in enumerate(groups):
                            bi = gemm_ctr[0] % 6; gemm_ctr[0] += 1
                            for k in range(kc):
                                S.op("pe", lambda: PE.matmul(PF[:rows, bi, 0:ns], lhsT=act[:, k, t0:t0 + rows], rhs=wt[:, koff[gi] + k, 0:ns],
                                                             start=(k == 0), stop=(k == kc - 1)),
                                     reads=[b_act, bw], writes=[pf[bi]], signal=(k == kc - 1))
                            aps.append(PF[:rows, bi, 0:ns]); bufs.append(pf[bi])
                        cb(t0, rows, n0, ns, aps, bufs)
                else:
                    for c0 in range(0, ns, 128):
                        for t0 in range(0, T, 352):
                            n = min(352, T - t0)
                            aps = []; bufs = []
                            for gi, (act, b_act, kc, W) in enumerate(groups):
                                bi = gemm_ctr[0] % 6; gemm_ctr[0] += 1
                                for k in range(kc):
                                    S.op("pe", lambda: PE.matmul(PF[:, bi, 0:n], lhsT=wt[:, koff[gi] + k, c0:c0 + 128], rhs=act[:, k, t0:t0 + n],
                                                                 start=(k == 0), stop=(k == kc - 1)),
                                         reads=[b_act, bw], writes=[pf[bi]], signal=(k == kc - 1))
                                aps.append(PF[:, bi, 0:n]); bufs.append(pf[bi])
                            cb(n0 + c0, t0, n, aps, bufs)

        def load_actT(dst_tile, b_dst, src, b_src, kc):
            v = src.rearrange("(kc p) t -> p kc t", p=128)
            slab_ctr[0] += 1
            tok = ("act", slab_ctr[0])
            for kk in range(0, kc, 8):
                k2 = min(kc, kk + 8)
                S.dma("sp", lambda: SP.dma_start(out=dst_tile[:, kk:k2, :], in_=v[:, kk:k2, :]), reads=[b_src], writes=[b_dst], tok=tok)

        def dump_bf16(src_ap, b_src, rows, cols, col0=0):
            with contextlib.ExitStack() as stk:
                t1, bt1 = sb(stk, "dmp1", [128, cols], BF16); t2, bt2 = sb(stk, "dmp2", [128, cols], F32)
                S.dma("sp", lambda: SP.dma_start(out=t1[:rows, :], in_=src_ap), reads=[b_src], writes=[bt1])
                S.op("dve", lambda: V.tensor_copy(out=t2[:rows, :], in_=t1[:rows, :]), reads=[bt1], writes=[bt2])
                S.dma("sp", lambda: SP.dma_start(out=dbg_out[:rows, col0:col0 + cols], in_=t2[:rows, :]), reads=[bt2], writes=[b_out])
                S.barrier()

        def dump_f32(src_ap, b_src, rows, cols, col0=0):
            with contextlib.ExitStack() as stk:
                t2, bt2 = sb(stk, "dmp3", [128, cols], F32)
                S.dma("sp", lambda: SP.dma_start(out=t2[:rows, :], in_=src_ap), reads=[b_src], writes=[bt2])
                S.dma("sp", lambda: SP.dma_start(out=dbg_out[:rows, col0:col0 + cols], in_=t2[:rows, :]), reads=[bt2], writes=[b_out])
                S.barrier()

        try:
            with contextlib.ExitStack() as stk:
              if not KSKIP12:
                norm_T(stk, x_own, Buf("x_in"), TL, D, gmix[0:1, :], nT_b, b_nT, "a")
                S.barrier()
            if not KSKIP12:
              S.collective(lambda: G.collective_compute("AllGather", ALU.bypass, replica_groups=[list(range(NC_))],
                                                      ins=[nT_b.opt()], outs=[G1.opt()]), reads=[b_nT], writes=[b_G1])
            if STOP == 1:
                dump_bf16(G1[3 * D:3 * D + 128, :], b_G1, 128, TL, 0)
                dump_bf16(G1[5 * D + 4096 - 128:5 * D + 4096, :], b_G1, 128, TL, 2048)
                raise _Stop()

            with contextlib.ExitStack() as stk:
                actA, b_actA = sb(stk, "actA", [128, 32, TL], BF16)
                w0, bw0 = sb(stk, "w0", [128, 32, 512], BF16); w1, bw1 = sb(stk, "w1", [128, 32, 512], BF16)
                ev0, b_ev0 = sb(stk, "ev0", [128, 512], F32); ev1, b_ev1 = sb(stk, "ev1", [128, 512], F32)
                evs = [(ev0, b_ev0), (ev1, b_ev1)]; ei = [0]
                for r in range(0 if KSKIP12 else NC_):
                    load_actT(actA, b_actA, G1[r * D:(r + 1) * D, :], b_G1, 32)

                    def cb(t0, rows, n0, ns, aps, bufs, r=r):
                        e, be = evs[ei[0] % 2]; ei[0] += 1
                        copy_any(e[:rows, :ns], aps[0], [bufs[0]], [be])
                        S.dma("sp", lambda: SP.dma_start(out=proj[r * TL + t0:r * TL + t0 + rows, n0:n0 + ns], in_=e[:rows, :ns]),
                              reads=[be], writes=[b_proj])
                    gemm("tm", [(actA, b_actA, 32, w_mine)], TL, NMINE, cb, [w0, w1], [bw0, bw1], 512)
                S.barrier()
            if STOP == 2:
                dump_f32(proj[2 * TL + 300:2 * TL + 428, 0:2176], b_proj, 128, 2176, 0)
                raise _Stop()

            with contextlib.ExitStack() as stk:
                gqb, b_gqb = sb(stk, "gqb", [128, 384], F32); gkb, b_gkb = sb(stk, "gkb", [128, 384], F32)
                ghb, b_ghb = sb(stk, "ghb", [128, 256], F32)
                lbb, b_lbb = sb(stk, "lbb", [128, 2, 256], F32); lb, b_lb = sb(stk, "lb", [128, 256], F32); oml, b_oml = sb(stk, "oml", [128, 256], F32)
                bp, b_bp = sb(stk, "bp", [128, 24 * 128], F32); bnw, b_bnw = sb(stk, "bnw", [128, 384], F32); bc, b_bc = sb(stk, "bc", [128, 168], F32)
                hmat, b_hmat = sb(stk, "hmat", [128, 768], F32)
                seqm_f, b_seqm_f = sb(stk, "seqm_f", [128, 2048], F32); seqm, b_seqm = sb(stk, "seqm", [128, 2048], BF16)
                rowm, b_rowm = sb(stk, "rowm", [128, 16], F32)
                onesb, b_onesb = sb(stk, "onesb", [128, 128], BF16)
                trib, b_trib = sb(stk, "trib", [128, 256], F32)
                S.dma("sp", lambda: SP.dma_start(out=gqb[:], in_=gq.partition_broadcast(128)), writes=[b_gqb])
                S.dma("sp", lambda: SP.dma_start(out=gkb[:], in_=gk.partition_broadcast(128)), writes=[b_gkb])
                for h in range(2):
                    S.dma("sp", lambda: SP.dma_start(out=ghb[:, h * 128:(h + 1) * 128], in_=ghg.partition_broadcast(128)), writes=[b_ghb])
                    S.dma("sp", lambda: SP.dma_start(out=lbb[:, h, :], in_=lbraw[h:h + 1, :].partition_broadcast(128)), writes=[b_lbb])
                S.dma("sp", lambda: SP.dma_start(out=bp[:], in_=bias_p), writes=[b_bp])
                S.dma("sp", lambda: SP.dma_start(out=bnw[:], in_=bias_new), writes=[b_bnw])
                S.dma("sp", lambda: SP.dma_start(out=bc[:], in_=bias_c), writes=[b_bc])
                S.dma("sp", lambda: SP.dma_start(out=hmat[:], in_=hmat_d), writes=[b_hmat])
                S.dma("sp", lambda: SP.dma_start(out=seqm_f[:], in_=seqmask_d), writes=[b_seqm_f])
                S.dma("sp", lambda: SP.dma_start(out=rowm[:], in_=rowmask_d), writes=[b_rowm])
                S.op("dve", lambda: V.tensor_copy(out=seqm[:], in_=seqm_f[:]), reads=[b_seqm_f], writes=[b_seqm])
                S.op("dve", lambda: V.tensor_copy(out=onesb[:], in_=hmat[:, 128:256]), reads=[b_hmat], writes=[b_onesb])
                S.op("dve", lambda: V.tensor_copy(out=trib[:, 0:128], in_=hmat[:, 0:128]), reads=[b_hmat], writes=[b_trib])
                S.op("dve", lambda: V.tensor_copy(out=trib[:, 128:256], in_=hmat[:, 384:512]), reads=[b_hmat], writes=[b_trib])
                S.op("dve", lambda: V.tensor_tensor(out=lb[:], in0=lbb[:, 0, :], in1=lbb[:, 1, :], op=ALU.subtract), reads=[b_lbb], writes=[b_lb])
                S.op("act", lambda: A.activation(out=lb[:], in_=lb[:], func=AF.Sigmoid), reads=[b_lb], writes=[b_lb])
                S.op("dve", lambda: V.tensor_scalar(out=oml[:], in0=lb[:], scalar1=-1.0, scalar2=1.0, op0=ALU.mult, op1=ALU.add), reads=[b_lb], writes=[b_oml])

                qkv, b_qkv = sb(stk, "qkv", [128, 1152], F32)
                sq, b_sq = sb(stk, "sq", [128, 768], F32); ssq, b_ssq = sb(stk, "ssq", [128, 6], F32)
                qkn, b_qkn = sb(stk, "qkn", [128, 768], BF16)
                kvo, b_kvo = sb(stk, "kvo", [128, 3, 256], F32)
                qT, b_qT = sb(stk, "qT", [128, 384], BF16)
                KT = []; VR = []
                for g in range(3):
                    KT.append(sb(stk, "KT%d" % g, [128, NKT[g], 128], BF16))
                    VR.append(sb(stk, "VR%d" % g, [128, NKT[g], 128], BF16))
                b_KT = [[Buf("kt%d_%d" % (g, i)) for i in range(NKT[g])] for g in range(3)]
                b_VR = [[Buf("vr%d_%d" % (g, i)) for i in range(NKT[g])] for g in range(3)]
                ssb = [sb(stk, "ssb%d" % i, [128, 512], F32) for i in range(2)]
                ptb = [sb(stk, "ptb%d" % i, [128, 512], BF16) for i in range(2)]
                rz, b_rz = sb(stk, "rz", [128, 128], F32); oTb, b_oTb = sb(stk, "oTb", [128, 128], BF16)
                ckf, b_ckf = sb(stk, "ckf", [128, 16, 256], BF16)
                ckT, b_ckT = sb(stk, "ckT", [128, 16, 128], BF16)
                chunk_ctr = [0]

                def qk_norm_and_T(rows, kvdst):
                    S.op("dve", lambda: V.tensor_tensor(out=sq[:rows, :], in0=qkv[:rows, 0:768], in1=qkv[:rows, 0:768], op=ALU.mult), reads=[b_qkv], writes=[b_sq])
                    S.op("dve", lambda: V.tensor_reduce(out=ssq[:rows, :], in_=sq[:rows, :].rearrange("p (a b) -> p a b", a=6), axis=AX.X, op=ALU.add),
                         reads=[b_sq], writes=[b_ssq])
                    S.op("dve", lambda: V.tensor_scalar(out=ssq[:rows, :], in0=ssq[:rows, :], scalar1=1.0 / 128, scalar2=EPS, op0=ALU.mult, op1=ALU.add),
                         reads=[b_ssq], writes=[b_ssq])
                    S.op("act", lambda: A.activation(out=ssq[:rows, :], in_=ssq[:rows, :], func=AF.Sqrt), reads=[b_ssq], writes=[b_ssq])
                    S.op("dve", lambda: V.reciprocal(out=ssq[:rows, :], in_=ssq[:rows, :]), reads=[b_ssq], writes=[b_ssq])
                    for j in range(6):
                        gsrc = gqb if j < 3 else gkb
                        gg = j % 3
                        S.op("dve", lambda: V.scalar_tensor_tensor(out=qkn[:rows, j * 128:(j + 1) * 128], in0=qkv[:rows, j * 128:(j + 1) * 128],
                                                                   scalar=ssq[:rows, j:j + 1], in1=gsrc[:rows, gg * 128:(gg + 1) * 128],
                                                                   op0=ALU.mult, op1=ALU.mult), reads=[b_qkv, b_ssq, b_gqb, b_gkb], writes=[b_qkn])
                        if j >= 3:
                            S.op("dve", lambda: V.scalar_tensor_tensor(out=kvo[:rows, gg, 0:128], in0=qkv[:rows, j * 128:(j + 1) * 128],
                                                                        scalar=ssq[:rows, j:j + 1], in1=gkb[:rows, gg * 128:(gg + 1) * 128],
                                                                        op0=ALU.mult, op1=ALU.mult), reads=[b_qkv, b_ssq, b_gkb], writes=[b_kvo])
                    for gg in range(3):
                        S.op("act", lambda: A.copy(out=kvo[:rows, gg, 128:256], in_=qkv[:rows, 768 + gg * 128:896 + gg * 128]), reads=[b_qkv], writes=[b_kvo])
                    for j in range(6):
                        S.op("pe", lambda: PE.transpose(out=PB[:, 0, j * 128:j * 128 + rows], in_=qkn[:rows, j * 128:(j + 1) * 128], identity=identb[:rows, :rows]),
                             reads=[b_qkn, b_identb], writes=[pb[0]], signal=(j == 5))

                def attn_chunk(nkt, kt_aps, kt_bufs, q_ap, q_buf, nq, bias_ap, bias_buf):
                    i = chunk_ctr[0] % 2; chunk_ctr[0] += 1
                    bank = 4 + i
                    for a in range(nkt):
                        S.op("pe", lambda: PE.matmul(PF[:, bank, a * nq:(a + 1) * nq], lhsT=kt_aps[a], rhs=q_ap, start=True, stop=True),
                             reads=[kt_bufs[a], q_buf], writes=[pf[bank]], signal=(a == nkt - 1))
                    sst, b_sst = ssb[i]; pt, b_pt = ptb[i]
                    S.op("dve", lambda: V.scalar_tensor_tensor(out=sst[:, 0:nkt * nq], in0=PF[:, bank, 0:nkt * nq], scalar=128.0 ** -0.5, in1=bias_ap,
                                                               op0=ALU.mult, op1=ALU.add), reads=[pf[bank], bias_buf], writes=[b_sst])
                    S.op("act", lambda: A.activation(out=pt[:, 0:nkt * nq], in_=sst[:, 0:nkt * nq], func=AF.Exp), reads=[b_sst], writes=[b_pt])
                    return pt, b_pt

                def attn_pv(pt, b_pt, nkt, nq, v_aps, v_bufs, col0, first, last):
                    for a in range(nkt):
                        f = first and a == 0; l = last and a == nkt - 1
                        S.op("pe", lambda: PE.matmul(PF[:, 2, col0:col0 + nq], lhsT=v_aps[a], rhs=pt[:, a * nq:(a + 1) * nq], start=f, stop=l),
                             reads=[v_bufs[a], b_pt], writes=[pf[2]], signal=False)
                        S.op("pe", lambda: PE.matmul(PF[:, 3, col0:col0 + nq], lhsT=onesb[:], rhs=pt[:, a * nq:(a + 1) * nq], start=f, stop=l),
                             reads=[b_onesb, b_pt], writes=[pf[3]], signal=True)

                def attn_finish(ncols, dst_list):
                    S.op("dve", lambda: V.reciprocal(out=rz[:, 0:ncols], in_=PF[:, 3, 0:ncols]), reads=[pf[3]], writes=[b_rz])
                    S.op("dve", lambda: V.tensor_tensor(out=oTb[:, 0:ncols], in0=PF[:, 2, 0:ncols], in1=rz[:, 0:ncols], op=ALU.mult),
                         reads=[pf[2], b_rz], writes=[b_oTb])
                    for (r0, c0, s0_, n) in dst_list:
                        S.dma("sp", lambda: SP.dma_start(out=ST[r0:r0 + 128, c0:c0 + n], in_=oTb[:, s0_:s0_ + n]), reads=[b_oTb], writes=[b_ST])

                def attn_prompt_tile(t):
                    r = t // 8; lt = (t % 8) * 128
                    S.dma("sp", lambda: SP.dma_start(out=qkv[:], in_=proj[r * TL + lt:r * TL + lt + 128, 0:1152]), reads=[b_proj], writes=[b_qkv])
                    qk_norm_and_T(128, None)
                    S.op("act", lambda: A.copy(out=qT[:], in_=PB[:, 0, 0:384]), reads=[pb[0]], writes=[b_qT])
                    for g in range(3):
                        slot = t % NKT[g]
                        S.op("dve", lambda: V.tensor_copy(out=KT[g][0][:, slot, :], in_=PB[:, 0, 384 + g * 128:512 + g * 128]), reads=[pb[0]], writes=[b_KT[g][slot]])
                        S.op("dve", lambda: V.tensor_copy(out=VR[g][0][:, slot, :], in_=qkv[:, 768 + g * 128:896 + g * 128]), reads=[b_qkv], writes=[b_VR[g][slot]])
                        W = GROUPS[g][0]
                        if t >= 64 - W // 128:
                            row0 = (0, 128, 640)[g] + (t - (64 - W // 128)) * 128
                            S.dma("sp", lambda: SP.dma_start(out=kvp_o[row0:row0 + 128, :], in_=kvo[:, g, :]), reads=[b_kvo], writes=[b_out])
                    pend = []
                    for g in range(3):
                        js = [j for j in range(NKT[g]) if t - j >= 0]
                        for c0 in range(0, len(js), 4):
                            cj = js[c0:c0 + 4]
                            slots = [(t - j) % NKT[g] for j in cj]
                            pend.append((g, cj, slots))
                    for ci, (g, cj, slots) in enumerate(pend):
                        pt, b_pt = attn_chunk(len(cj), [KT[g][0][:, s_, :] for s_ in slots], [b_KT[g][s_] for s_ in slots], qT[:, g * 128:(g + 1) * 128], b_qT, 128,
                                              bp[:, (KOFF[g] + cj[0]) * 128:(KOFF[g] + cj[0] + len(cj)) * 128], b_bp)
                        attn_pv(pt, b_pt, len(cj), 128, [VR[g][0][:, s_, :] for s_ in slots], [b_VR[g][s_] for s_ in slots], 0, ci == 0, ci == len(pend) - 1)
                    attn_finish(128, [(r * 384, lt, 0, 128)])

                for u in range(2 if "b" in KPARTS else 0):
                    for rr in range(4):
                        rk = 4 * u + rr
                        S.dma("sp", lambda: SP.dma_start(out=qkv[rr * 32:(rr + 1) * 32, :], in_=proj[rk * TL + 1024:rk * TL + 1056, 0:1152]),
                              reads=[b_proj], writes=[b_qkv])
                    qk_norm_and_T(128, None)
                    S.op("act", lambda: A.copy(out=qT[:], in_=PB[:, 0, 0:384]), reads=[pb[0]], writes=[b_qT])
                    for g in range(3):
                        S.op("dve", lambda: V.tensor_copy(out=KT[g][0][:, 0, :], in_=PB[:, 0, 384 + g * 128:512 + g * 128]), reads=[pb[0]], writes=[b_KT[g][0]])
                        S.op("dve", lambda: V.tensor_copy(out=VR[g][0][:, 0, :], in_=qkv[:, 768 + g * 128:896 + g * 128]), reads=[b_qkv], writes=[b_VR[g][0]])
                        S.dma("sp", lambda: SP.dma_start(out=kvs_o[g, u * 128:(u + 1) * 128, :], in_=kvo[:, g, :]), reads=[b_kvo], writes=[b_out])
                    for g in range(3):
                        pt, b_pt = attn_chunk(1, [KT[g][0][:, 0, :]], [b_KT[g][0]], qT[:, g * 128:(g + 1) * 128], b_qT, 128, bnw[:, g * 128:(g + 1) * 128], b_bnw)
                        attn_pv(pt, b_pt, 1, 128, [VR[g][0][:, 0, :]], [b_VR[g][0]], 0, g == 0, False)
                    for bl in range(16):
                        b = 16 * u + bl
                        for g in range(3):
                            nct = CT[g]
                            S.dma("pool", lambda: G.dma_start(out=ckf[:, 0:nct, :], in_=ck[g][b].rearrange("(j p) f -> p j f", p=128)), writes=[b_ckf])
                            for j0 in range(0, nct, 8):
                                nj = min(8, nct - j0)
                                for j in range(nj):
                                    S.op("pe", lambda: PE.transpose(out=PB[:, 1, j * 128:(j + 1) * 128], in_=ckf[:, j0 + j, 0:128], identity=identb[:]),
                                         reads=[b_ckf, b_identb], writes=[pb[1]], signal=(j == nj - 1))
                                copy_any(ckT[:, j0:j0 + nj, :], PB[:, 1, 0:nj * 128].rearrange("p (a b) -> p a b", a=nj), [pb[1]], [b_ckT])
                            for j0 in range(0, nct, 4):
                                nj = min(4, nct - j0)
                                last = (bl == 15 and g == 2 and j0 + nj == nct)
                                pt, b_pt = attn_chunk(nj, [ckT[:, j0 + a, :] for a in range(nj)], [b_ckT] * nj, qT[:, g * 128 + bl * 8:g * 128 + bl * 8 + 8], b_qT, 8,
                                                      bc[:, (COFF[g] + j0) * 8:(COFF[g] + j0 + nj) * 8], b_bc)
                                attn_pv(pt, b_pt, nj, 8, [ckf[:, j0 + a, 128:256] for a in range(nj)], [b_ckf] * nj, bl * 8, False, last)
                    attn_finish(128, [((4 * u + rr) * 384, 1024, rr * 32, 32) for rr in range(4)])

                hg, b_hg = sb(stk, "hg", [128, 1024], F32)
                def t256(name, dt=F32):
                    return sb(stk, name, [128, 256], dt)
                sgq, b_sgq = t256("sgq"); qf, b_qf = t256("qf"); sg, b_sg = t256("sg"); tmpg, b_tmpg = t256("tmpg"); gate, b_gate = t256("gate")
                kk_, b_kk = t256("kk"); lg, b_lg = t256("lg"); vbf, b_vbf = t256("vbf", BF16)
                bmid, b_bmid = t256("bmid"); d2, b_d2 = t256("d2"); d3, b_d3 = t256("d3"); e1, b_e1 = t256("e1"); e2, b_e2 = t256("e2"); e2n, b_e2n = t256("e2n"); e3, b_e3 = t256("e3")
                qd1, b_qd1 = t256("qd1", BF16); qd2, b_qd2 = t256("qd2", BF16); kd2, b_kd2 = t256("kd2", BF16); kd3, b_kd3 = t256("kd3", BF16)
                dcol, b_dcol = sb(stk, "dcol", [128, 64], F32)
                T3, b_T3 = sb(stk, "T3", [128, 768], BF16)
                Abf, b_Abf = sb(stk, "Abf", [128, 128], BF16)
                Sst, b_Sst = sb(stk, "Sst", [128, 2, 128], F32); Sbf, b_Sbf = sb(stk, "Sbf", [128, 2, 128], BF16)
                osq, b_osq = sb(stk, "osq", [128, 128], F32); oss, b_oss = sb(stk, "oss", [128, 1], F32)
                sgo, b_sgo = t256("sgo"); on_, b_on = sb(stk, "on", [128, 128], F32); onb, b_onb = sb(stk, "onb", [128, 128], BF16)
                ohT, b_ohT = sb(stk, "ohT", [128, 128], BF16)
                qm, b_qm = sb(stk, "qm", [128, 128], BF16); km, b_km = sb(stk, "km", [128, 128], BF16)
                s0f, b_s0f = sb(stk, "s0f", [128, 128], F32); s0b, b_s0b = sb(stk, "s0b", [128, 128], BF16); s1f, b_s1f = sb(stk, "s1f", [128, 128], F32)
                ones1, b_ones1 = sb(stk, "ones1", [128, 2], F32)
                S.op("dve", lambda: V.tensor_copy(out=ones1[:], in_=hmat[:, 128:130]), reads=[b_hmat], writes=[b_ones1])
                S.op("dve", lambda: V.memset(Sst[:], 0.0), writes=[b_Sst])
                S.op("dve", lambda: V.memset(Sbf[:], 0.0), writes=[b_Sbf])

                def hgrn_tile(sample, dst_list, u=0):
                    mo = 384 if sample else 0
                    tri_ap = hmat[:, mo:mo + 128]; ones_ap = hmat[:, mo + 128:mo + 256]; mid_ap = hmat[:, (640 if sample else 256):(768 if sample else 384)]
                    cm = trib[:, 128:256] if sample else trib[:, 0:128]
                    S.op("act", lambda: A.activation(out=sgq[:], in_=hg[:, 0:256], func=AF.Sigmoid), reads=[b_hg], writes=[b_sgq])
                    S.op("dve", lambda: V.scalar_tensor_tensor(out=qf[:], in0=hg[:, 0:256], scalar=128.0 ** -0.5, in1=sgq[:], op0=ALU.mult, op1=ALU.mult),
                         reads=[b_hg, b_sgq], writes=[b_qf])
                    S.op("act", lambda: A.activation(out=sg[:], in_=hg[:, 256:512], func=AF.Sigmoid), reads=[b_hg], writes=[b_sg])
                    S.op("dve", lambda: V.tensor_tensor(out=tmpg[:], in0=sg[:], in1=oml[:], op=ALU.mult), reads=[b_sg, b_oml], writes=[b_tmpg])
                    S.op("dve", lambda: V.tensor_tensor(out=gate[:], in0=tmpg[:], in1=lb[:], op=ALU.add), reads=[b_tmpg, b_lb], writes=[b_gate])
                    S.op("dve", lambda: V.tensor_tensor(out=kk_[:], in0=oml[:], in1=tmpg[:], op=ALU.subtract), reads=[b_tmpg, b_oml], writes=[b_kk])
                    S.op("act", lambda: A.activation(out=lg[:], in_=gate[:], func=AF.Ln), reads=[b_gate], writes=[b_lg])
                    S.op("act", lambda: A.copy(out=vbf[:], in_=hg[:, 512:768]), reads=[b_hg], writes=[b_vbf])
                    S.op("act", lambda: A.activation(out=sgo[:], in_=hg[:, 768:1024], func=AF.Sigmoid), reads=[b_hg], writes=[b_sgo])
                    S.op("pe", lambda: PE.matmul(PF[:, 0, 0:256], lhsT=tri_ap, rhs=lg[:], start=True, stop=True), reads=[b_hmat, b_lg], writes=[pf[0]])
                    S.op("pe", lambda: PE.matmul(PF[:, 0, 256:512], lhsT=ones_ap, rhs=lg[:], start=True, stop=True), reads=[b_hmat, b_lg], writes=[pf[0]])
                    S.op("pe", lambda: PE.matmul(PF[:, 1, 0:256], lhsT=mid_ap, rhs=lg[:], start=True, stop=True), reads=[b_hmat, b_lg], writes=[pf[1]])
                    nsq = 16 if sample else 2
                    for h in range(2):
                        rhs = rowm[:, 0:16] if sample else ones1[:, 0:2]
                        S.op("pe", lambda: PE.matmul(PF[:, 1, 256 + h * 16:256 + h * 16 + nsq], lhsT=lg[:, h * 128:(h + 1) * 128], rhs=rhs, start=True, stop=True),
                             reads=[b_lg, b_rowm, b_ones1], writes=[pf[1]])
                    S.op("act", lambda: A.activation(out=dcol[:, 0:32], in_=PF[:, 1, 256:288], func=AF.Exp), reads=[pf[1]], writes=[b_dcol])
                    S.op("act", lambda: A.copy(out=bmid[:], in_=PF[:, 1, 0:256]), reads=[pf[1]], writes=[b_bmid])
                    S.op("act", lambda: A.activation(out=e1[:], in_=PF[:, 0, 0:256], func=AF.Exp), reads=[pf[0]], writes=[b_e1])
                    S.op("dve", lambda: V.tensor_tensor(out=d2[:], in0=PF[:, 0, 0:256], in1=bmid[:], op=ALU.subtract), reads=[pf[0], b_bmid], writes=[b_d2])
                    S.op("act", lambda: A.copy(out=d3[:], in_=PF[:, 0, 256:512]), reads=[pf[0]], writes=[b_d3])
                    S.op("dve", lambda: V.tensor_tensor(out=d3[:], in0=d3[:], in1=PF[:, 0, 0:256], op=ALU.subtract), reads=[pf[0], b_d3], writes=[b_d3])
                    S.op("act", lambda: A.activation(out=e2[:], in_=d2[:], func=AF.Exp), reads=[b_d2], writes=[b_e2])
                    S.op("act", lambda: A.activation(out=e2n[:], in_=d2[:], func=AF.Exp, scale=-1.0), reads=[b_d2], writes=[b_e2n])
                    S.op("act", lambda: A.activation(out=e3[:], in_=d3[:], func=AF.Exp), reads=[b_d3], writes=[b_e3])
                    S.op("dve", lambda: V.tensor_tensor(out=qd1[:], in0=qf[:], in1=e1[:], op=ALU.mult), reads=[b_qf, b_e1], writes=[b_qd1])
                    S.op("dve", lambda: V.tensor_tensor(out=qd2[:], in0=qf[:], in1=e2[:], op=ALU.mult), reads=[b_qf, b_e2], writes=[b_qd2])
                    S.op("dve", lambda: V.tensor_tensor(out=kd2[:], in0=kk_[:], in1=e2n[:], op=ALU.mult), reads=[b_kk, b_e2n], writes=[b_kd2])
                    S.op("dve", lambda: V.tensor_tensor(out=kd3[:], in0=kk_[:], in1=e3[:], op=ALU.mult), reads=[b_kk, b_e3], writes=[b_kd3])
                    for h in range(2):
                        hs = slice(h * 128, (h + 1) * 128)
                        for j, (src, bsrc) in enumerate(((qd1, b_qd1), (qd2, b_qd2), (kd2, b_kd2))):
                            S.op("pe", lambda: PE.transpose(out=PB[:, 1, j * 128:(j + 1) * 128], in_=src[:, hs], identity=identb[:]),
                                 reads=[bsrc, b_identb], writes=[pb[1]], signal=(j == 2))
                        S.op("act", lambda: A.copy(out=T3[:, 0:384], in_=PB[:, 1, 0:384]), reads=[pb[1]], writes=[b_T3])
                        S.op("pe", lambda: PE.matmul(PF[:, 2, 0:128], lhsT=T3[:, 256:384], rhs=T3[:, 128:256], start=True, stop=True), reads=[b_T3], writes=[pf[2]])
                        S.op("dve", lambda: V.tensor_tensor(out=Abf[:], in0=PF[:, 2, 0:128], in1=cm, op=ALU.mult), reads=[pf[2], b_trib], writes=[b_Abf])
                        if not sample:
                            S.op("pe", lambda: PE.matmul(PF[:, 3, 0:128], lhsT=T3[:, 0:128], rhs=Sbf[:, h, :], start=True, stop=False),
                                 reads=[b_T3, b_Sbf], writes=[pf[3]], signal=False)
                            S.op("pe", lambda: PE.matmul(PF[:, 3, 0:128], lhsT=Abf[:], rhs=vbf[:, hs], start=False, stop=True), reads=[b_Abf, b_vbf], writes=[pf[3]])
                            S.op("pe", lambda: PE.matmul(PF[:, 2, 128:256], lhsT=kd3[:, hs], rhs=vbf[:, hs], start=True, stop=True), reads=[b_kd3, b_vbf, b_Abf], writes=[pf[2]])
                            S.op("dve", lambda: V.scalar_tensor_tensor(out=Sst[:, h, :], in0=Sst[:, h, :], scalar=dcol[:, h * 16:h * 16 + 1], in1=PF[:, 2, 128:256],
                                                                       op0=ALU.mult, op1=ALU.add), reads=[b_Sst, b_dcol, pf[2]], writes=[b_Sst])
                            S.op("act", lambda: A.copy(out=Sbf[:, h, :], in_=Sst[:, h, :]), reads=[b_Sst], writes=[b_Sbf])
                        else:
                            S.op("pe", lambda: PE.matmul(PF[:, 3, 0:128], lhsT=Abf[:], rhs=vbf[:, hs], start=True, stop=False), reads=[b_Abf, b_vbf], writes=[pf[3]])
                            for bl in range(16):
                                b = 16 * u + bl
                                S.dma("sp", lambda: SP.dma_start(out=s0f[:], in_=s0[b, h]), writes=[b_s0f])
                                S.op("act", lambda: A.copy(out=s0b[:], in_=s0f[:]), reads=[b_s0f], writes=[b_s0b])
                                S.op("dve", lambda: V.tensor_tensor(out=qm[:], in0=T3[:, 0:128], in1=seqm[:, bl * 128:(bl + 1) * 128], op=ALU.mult),
                                     reads=[b_T3, b_seqm], writes=[b_qm])
                                S.op("pe", lambda: PE.matmul(PF[:, 3, 0:128], lhsT=qm[:], rhs=s0b[:], start=False, stop=(bl == 15)), reads=[b_qm, b_s0b], writes=[pf[3]])
                                S.op("dve", lambda: V.tensor_scalar(out=km[:], in0=kd3[:, hs], scalar1=rowm[:, bl:bl + 1], scalar2=None, op0=ALU.mult),
                                     reads=[b_kd3, b_rowm], writes=[b_km])
                                S.op("pe", lambda: PE.matmul(PF[:, 2, 128:256], lhsT=km[:], rhs=vbf[:, hs], start=True, stop=True), reads=[b_km, b_vbf, b_Abf], writes=[pf[2]])
                                S.op("dve", lambda: V.scalar_tensor_tensor(out=s1f[:], in0=s0f[:], scalar=dcol[:, h * 16 + bl:h * 16 + bl + 1], in1=PF[:, 2, 128:256],
                                                                           op0=ALU.mult, op1=ALU.add), reads=[b_s0f, b_dcol, pf[2]], writes=[b_s1f])
                                S.dma("sp", lambda: SP.dma_start(out=hgs_o[b, h], in_=s1f[:]), reads=[b_s1f], writes=[b_out])
                        S.op("act", lambda: A.activation(out=osq[:], in_=PF[:, 3, 0:128], func=AF.Square, accum_out=oss[:]), reads=[pf[3]], writes=[b_osq, b_oss])
                        S.op("dve", lambda: V.tensor_scalar(out=oss[:], in0=oss[:], scalar1=1.0 / 128, scalar2=EPS, op0=ALU.mult, op1=ALU.add), reads=[b_oss], writes=[b_oss])
                        S.op("act", lambda: A.activation(out=oss[:], in_=oss[:], func=AF.Sqrt), reads=[b_oss], writes=[b_oss])
                        S.op("dve", lambda: V.reciprocal(out=oss[:], in_=oss[:]), reads=[b_oss], writes=[b_oss])
                        S.op("dve", lambda: V.scalar_tensor_tensor(out=on_[:], in0=PF[:, 3, 0:128], scalar=oss[:, 0:1], in1=ghb[:, hs], op0=ALU.mult, op1=ALU.mult),
                             reads=[pf[3], b_oss, b_ghb], writes=[b_on])
                        S.op("dve", lambda: V.tensor_tensor(out=onb[:], in0=on_[:], in1=sgo[:, hs], op=ALU.mult), reads=[b_on, b_sgo], writes=[b_onb])
                        S.op("pe", lambda: PE.transpose(out=PB[:, 0, 0:128], in_=onb[:], identity=identb[:]), reads=[b_onb, b_identb], writes=[pb[0]])
                        S.op("act", lambda: A.copy(out=ohT[:], in_=PB[:, 0, 0:128]), reads=[pb[0]], writes=[b_ohT])
                        for (r0, c0, s0_, n) in dst_list:
                            S.dma("sp", lambda: SP.dma_start(out=ST[r0 + 128 + h * 128:r0 + 256 + h * 128, c0:c0 + n], in_=ohT[:, s0_:s0_ + n]),
                                  reads=[b_ohT], writes=[b_ST])

                def hgrn_prompt_tile(t):
                    r = t // 8; lt = (t % 8) * 128
                    S.dma("sp", lambda: SP.dma_start(out=hg[:], in_=proj[r * TL + lt:r * TL + lt + 128, 1152:2176]), reads=[b_proj], writes=[b_hg])
                    hgrn_tile(False, [(r * 384, lt, 0, 128)])

                for t in range(KNT):
                    if "a" in KPARTS:
                        attn_prompt_tile(t)
                    if "c" in KPARTS:
                        hgrn_prompt_tile(t)
                for h in range(2):
                    S.dma("sp", lambda: SP.dma_start(out=hgp_o[h], in_=Sst[:, h, :]), reads=[b_Sst], writes=[b_out])
                for u in range(2 if "d" in KPARTS else 0):
                    for rr in range(4):
                        rk = 4 * u + rr
                        S.dma("sp", lambda: SP.dma_start(out=hg[rr * 32:(rr + 1) * 32, :], in_=proj[rk * TL + 1024:rk * TL + 1056, 1152:2176]),
                              reads=[b_proj], writes=[b_hg])
                    hgrn_tile(True, [((4 * u + rr) * 384, 1024, rr * 32, 32) for rr in range(4)], u=u)
                S.barrier()

            if STOP == 4:
                dump_bf16(ST[2 * 384:2 * 384 + 128, :], b_ST, 128, TL, 0)
                dump_bf16(ST[2 * 384 + 128:2 * 384 + 256, :], b_ST, 128, TL, 1056)
                dump_bf16(ST[2 * 384 + 256:2 * 384 + 384, :], b_ST, 128, TL, 2112)
                raise _Stop()
            S.collective(lambda: G.collective_compute("AllGather", ALU.bypass, replica_groups=[list(range(NC_))],
                                                      ins=[ST.opt()], outs=[G2.opt()]), reads=[b_ST], writes=[b_G2])

            w_gates = din("w_gates", [D, 2 * D]); w_ua = din("w_ua", [1024, D]); w_uh = din("w_uh", [2048, D])
            with contextlib.ExitStack() as stk:
                actA, b_actA = sb(stk, "actA6", [128, 32, TL], BF16)
                actB, b_actB = sb(stk, "actB6", [128, 24, TL], BF16)
                gidx, b_gidx = sb(stk, "gidx", [128, 24], I32)
                w0, bw0 = sb(stk, "w60", [128, 88, 128], BF16); w1, bw1 = sb(stk, "w61", [128, 88, 128], BF16)
                sga, b_sga = sb(stk, "sga", [128, 512], F32); sgb, b_sgb = sb(stk, "sgb", [128, 512], F32)
                m1, b_m1 = sb(stk, "m1", [128, 512], F32); m2, b_m2 = sb(stk, "m2", [128, 512], BF16)
                S.dma("sp", lambda: SP.dma_start(out=gidx[:], in_=gidx_d), writes=[b_gidx])
                load_actT(actA, b_actA, nT_b, b_nT, 32)
                for kc in range(24):
                    S.dma("pool", lambda: G.indirect_dma_start(out=actB[:, kc, :], out_offset=None, in_=G2[:, :],
                                                               in_offset=bass.IndirectOffsetOnAxis(ap=gidx[:, kc:kc + 1], axis=0)),
                          reads=[b_G2, b_gidx], writes=[b_actB])

                def cb6(n0, t0, n, aps, bufs):
                    S.op("act", lambda: A.activation(out=sga[:, :n], in_=aps[0], func=AF.Sigmoid), reads=[bufs[0]], writes=[b_sga])
                    S.op("act", lambda: A.activation(out=sgb[:, :n], in_=aps[1], func=AF.Sigmoid), reads=[bufs[1]], writes=[b_sgb])
                    S.op("dve", lambda: V.tensor_tensor(out=m1[:, :n], in0=aps[2], in1=sga[:, :n], op=ALU.mult), reads=[bufs[2], b_sga], writes=[b_m1])
                    S.op("dve", lambda: V.tensor_tensor(out=sgb[:, :n], in0=aps[3], in1=sgb[:, :n], op=ALU.mult), reads=[bufs[3], b_sgb], writes=[b_sgb])
                    S.op("dve", lambda: V.tensor_tensor(out=m2[:, :n], in0=m1[:, :n], in1=sgb[:, :n], op=ALU.add), reads=[b_m1, b_sgb], writes=[b_m2])
                    S.dma("sp", lambda: SP.dma_start(out=mT[n0:n0 + 128, t0:t0 + n], in_=m2[:, :n]), reads=[b_m2], writes=[b_mT])
                gemm("fm", [(actA, b_actA, 32, w_gates[:, 0:D]), (actA, b_actA, 32, w_gates[:, D:2 * D]),
                            (actB[:, 0:8, :], b_actB, 8, w_ua), (actB[:, 8:24, :], b_actB, 16, w_uh)],
                     TL, D, cb6, [w0, w1], [bw0, bw1], 128)
                S.barrier()

            if STOP == 6:
                dump_bf16(mT[0:128, :], b_mT, 128, TL, 0)
                dump_bf16(mT[D - 128:D, :], b_mT, 128, TL, 2048)
                raise _Stop()
            w_out = din("w_out", [D, D])
            with contextlib.ExitStack() as stk:
                actA, b_actA = sb(stk, "actA7", [128, 32, TL], BF16)
                w0, bw0 = sb(stk, "w70", [128, 32, 512], BF16); w1, bw1 = sb(stk, "w71", [128, 32, 512], BF16)
                xr = [sb(stk, "xr%d" % i, [128, 512], F32) for i in range(2)]; xi = [0]
                load_actT(actA, b_actA, mT, b_mT, 32)
                b_xin = Buf("x_in7")

                def cb7(t0, rows, n0, ns, aps, bufs):
                    e, be = xr[xi[0] % 2]; xi[0] += 1
                    S.dma("sp", lambda: SP.dma_start(out=e[:rows, :ns], in_=x_own[t0:t0 + rows, n0:n0 + ns]), reads=[b_xin], writes=[be])
                    S.op("dve", lambda: V.tensor_tensor(out=e[:rows, :ns], in0=aps[0], in1=e[:rows, :ns], op=ALU.add), reads=[bufs[0], be], writes=[be])
                    S.dma("sp", lambda: SP.dma_start(out=xacc[t0:t0 + rows, n0:n0 + ns], in_=e[:rows, :ns]), reads=[be], writes=[b_xacc])
                gemm("tm", [(actA, b_actA, 32, w_out)], TL, D, cb7, [w0, w1], [bw0, bw1], 512)
                S.barrier()
            with contextlib.ExitStack() as stk:
                norm_T(stk, xacc, b_xacc, TL, D, gffn[0:1, :], hT, b_hT, "b")
                S.barrier()

            if STOP == 7:
                dump_f32(xacc[0:128, :], b_xacc, 128, D, 0)
                raise _Stop()
            w_ffu = din("w_ffu", [D, 4 * D]); w_ffd = din("w_ffd", [4 * D, D])
            with contextlib.ExitStack() as stk:
                actA, b_actA = sb(stk, "actA8", [128, 32, TL], BF16)
                actB, b_actB = sb(stk, "actB8", [128, 32, TL], BF16)
                w0, bw0 = sb(stk, "w80", [128, 32, 256], BF16); w1, bw1 = sb(stk, "w81", [128, 32, 256], BF16)
                xr = [sb(stk, "xr8%d" % i, [128, 256], F32) for i in range(2)]; xi = [0]
                rl, b_rl = sb(stk, "rl", [128, 512], F32)
                load_actT(actA, b_actA, hT, b_hT, 32)
                for hb in range(4):
                    def cbu(n0, t0, n, aps, bufs):
                        kc = n0 // 128
                        S.op("act", lambda: A.activation(out=rl[:, :n], in_=aps[0], func=AF.Relu), reads=[bufs[0]], writes=[b_rl])
                        S.op("dve", lambda: V.tensor_tensor(out=actB[:, kc, t0:t0 + n], in0=rl[:, :n], in1=rl[:, :n], op=ALU.mult), reads=[b_rl], writes=[b_actB])
                    gemm("fm", [(actA, b_actA, 32, w_ffu[:, hb * D:(hb + 1) * D])], TL, D, cbu, [w0, w1], [bw0, bw1], 256)

                    def cbd(t0, rows, n0, ns, aps, bufs):
                        e, be = xr[xi[0] % 2]; xi[0] += 1
                        S.dma("sp", lambda: SP.dma_start(out=e[:rows, :ns], in_=xacc[t0:t0 + rows, n0:n0 + ns]), reads=[b_xacc], writes=[be])
                        S.op("dve", lambda: V.tensor_tensor(out=e[:rows, :ns], in0=aps[0], in1=e[:rows, :ns], op=ALU.add), reads=[bufs[0], be], writes=[be])
                        S.dma("sp", lambda: SP.dma_start(out=xacc[t0:t0 + rows, n0:n0 + ns], in_=e[:rows, :ns]), reads=[be], writes=[b_xacc])
                    gemm("tm", [(actB, b_actB, 32, w_ffd[hb * D:(hb + 1) * D, :])], TL, D, cbd, [w0, w1], [bw0, bw1], 256)
                S.barrier()

            if STOP == 8:
                dump_f32(xacc[0:128, :], b_xacc, 128, D, 0)
                raise _Stop()
            w_ple = din("w_ple", [256, D]); w_pg = din("w_pg", [D, D])
            with contextlib.ExitStack() as stk:
                norm_T(stk, xacc, b_xacc, TL, D, None, x2T, b_x2T, "c")
                S.barrier()
            with contextlib.ExitStack() as stk:
                norm_T(stk, p_own, Buf("p_in"), TL, 256, None, pT, b_pT, "d")
                S.barrier()
            with contextlib.ExitStack() as stk:
                actA, b_actA = sb(stk, "actA9", [128, 32, TL], BF16)
                actP, b_actP = sb(stk, "actP9", [128, 2, TL], BF16)
                w0, bw0 = sb(stk, "w90", [128, 34, 512], BF16); w1, bw1 = sb(stk, "w91", [128, 34, 512], BF16)
                xr = [sb(stk, "xr9%d" % i, [128, 512], F32) for i in range(2)]; xi = [0]
                sgg, b_sgg = sb(stk, "sgg", [128, 512], F32)
                load_actT(actA, b_actA, x2T, b_x2T, 32)
                load_actT(actP, b_actP, pT, b_pT, 2)

                def cb9(t0, rows, n0, ns, aps, bufs):
                    e, be = xr[xi[0] % 2]; xi[0] += 1
                    S.dma("sp", lambda: SP.dma_start(out=e[:rows, :ns], in_=xacc[t0:t0 + rows, n0:n0 + ns]), reads=[b_xacc], writes=[be])
                    S.op("act", lambda: A.activation(out=sgg[:rows, :ns], in_=aps[0], func=AF.Sigmoid), reads=[bufs[0]], writes=[b_sgg])
                    S.op("dve", lambda: V.tensor_tensor(out=sgg[:rows, :ns], in0=aps[1], in1=sgg[:rows, :ns], op=ALU.mult), reads=[bufs[1], b_sgg], writes=[b_sgg])
                    S.op("dve", lambda: V.tensor_tensor(out=e[:rows, :ns], in0=e[:rows, :ns], in1=sgg[:rows, :ns], op=ALU.add), reads=[be, b_sgg], writes=[be])
                    S.dma("sp", lambda: SP.dma_start(out=y_o[t0:t0 + rows, n0:n0 + ns], in_=e[:rows, :ns]), reads=[be], writes=[b_out])
                gemm("tm", [(actA, b_actA, 32, w_pg), (actP, b_actP, 2, w_ple)], TL, D, cb9, [w0, w1], [bw0, bw1], 512)
        except _Stop:
            pass
        S.barrier()
        print("instructions:", S.ninst, "sem counts:", {k: v for k, v in S.cnt.items()})
        nc._k_log = S.log
    return nc


def _consts(c):
    slopes = np.exp2(-8.0 * np.arange(1, 25, dtype=np.float64) / 24).reshape(3, 8)
    s = np.arange(128)[:, None]; q = np.arange(128)[None, :]
    bias_p = np.full((128, 24, 128), NEG, np.float32)
    bias_new = np.full((128, 3, 128), NEG, np.float32)
    bias_c = np.full((128, 21, 8), NEG, np.float32)
    for g, (W, d) in enumerate(GROUPS):
        sl = slopes[g, c]
        for j in range(NKT[g]):
            delta = j * 128 + q - s
            ok = (delta >= 0) & (delta <= W) & (delta % d == 0)
            bias_p[:, KOFF[g] + j, :] = np.where(ok, -sl * delta, NEG)
        dl = (q % 8) - (s % 8)
        ok = ((q // 8) == (s // 8)) & (dl >= 0) & (dl % d == 0)
        bias_new[:, g, :] = np.where(ok, -sl * dl, NEG)
        i8 = np.arange(8)[None, :]
        for j in range(CT[g]):
            l = j * 128 + s
            dist = W + i8 - l
            ok = (dist <= W) & (dist % d == 0)
            bias_c[:, COFF[g] + j, :] = np.where(ok, -sl * dist, NEG)
    tri = (s <= q).astype(np.float32)
    ones = np.ones((128, 128), np.float32)
    mid = np.broadcast_to((s <= 63), (128, 128)).astype(np.float32)
    same = ((s // 8) == (q // 8))
    triS = (same & (s <= q)).astype(np.float32)
    onesS = same.astype(np.float32)
    hmat = np.concatenate([tri, ones, mid, triS, onesS, np.zeros((128, 128), np.float32)], axis=1)
    seqmask = np.zeros((128, 16, 128), np.float32)
    for b in range(16):
        seqmask[:, b, b * 8:(b + 1) * 8] = 1.0
    rowmask = ((np.arange(128)[:, None] // 8) == np.arange(16)[None, :]).astype(np.float32)
    gidx = np.zeros((128, 24), np.int32)
    p = np.arange(128)
    for kc in range(24):
        if kc < 8:
            src, j = kc, 0
        else:
            src, j = (kc - 8) // 2, 1 + (kc - 8) % 2
        gidx[:, kc] = (src * 8 + c) * 384 + j * 128 + p
    return dict(bias_p=bias_p.reshape(128, -1), bias_new=bias_new.reshape(128, -1), bias_c=bias_c.reshape(128, -1),
                hmat=hmat, seqmask=seqmask.reshape(128, -1), rowmask=rowmask, gidx=gidx,
                ident=np.eye(128, dtype=np.float32))


_NC_CACHE = {}


def kernel(x_prompt, x_sample, cache_kv_w128, cache_kv_w512, cache_kv_w2048, state_hgrn, p_prompt, p_sample,
           g_mix, w_in, g_q, g_k, hg_lb_raw, g_hg_out, w_up_attn, w_up_hgrn, w_out, g_ffn, w_ff_up, w_ff_down,
           w_ple, w_ple_gate):
    f = lambda a: np.ascontiguousarray(np.asarray(a, dtype=np.float32))
    x_prompt, x_sample, p_prompt, p_sample = f(x_prompt), f(x_sample), f(p_prompt), f(p_sample)
    w_in0 = np.asarray(w_in, dtype=np.float32)[0]
    caches = [np.asarray(cache_kv_w128)[0], np.asarray(cache_kv_w512)[0], np.asarray(cache_kv_w2048)[0]]
    st = np.asarray(state_hgrn)[0]
    if "nc" not in _NC_CACHE:
        _NC_CACHE["nc"] = build()
    nc = _NC_CACHE["nc"]
    shared = dict(
        gmix=f(g_mix)[0:1], gffn=f(g_ffn)[0:1], w_gates=f(w_in0[:, 17408:25600]),
        w_ua=f(w_up_attn)[0], w_uh=f(w_up_hgrn)[0], w_out=f(w_out)[0], w_ffu=f(w_ff_up)[0], w_ffd=f(w_ff_down)[0],
        w_ple=f(w_ple)[0], w_pg=f(w_ple_gate)[0],
        gq=f(g_q)[0].reshape(1, 384), gk=f(g_k)[0].reshape(1, 384), ghg=f(g_hg_out)[0].reshape(1, 128),
    )
    in_maps = []
    for c in range(NC_):
        cols = []
        for base in (0, 3072, 6144):
            for g in range(3):
                o = base + g * 1024 + c * 128
                cols.append(np.arange(o, o + 128))
        for base in (9216, 11264, 13312, 15360):
            cols.append(np.arange(base + 256 * c, base + 256 * c + 256))
        cols = np.concatenate(cols)
        m = dict(shared)
        m["x_own"] = np.concatenate([x_prompt[0, 1024 * c:1024 * (c + 1)], x_sample[4 * c:4 * c + 4].reshape(32, D)], axis=0)
        m["p_own"] = np.concatenate([p_prompt[0, 0, 1024 * c:1024 * (c + 1)], p_sample[0, 4 * c:4 * c + 4].reshape(32, 256)], axis=0)
        m["w_mine"] = f(w_in0[:, cols])
        m["lbraw"] = f(np.asarray(hg_lb_raw)[:, 256 * c:256 * (c + 1)])
        for g in range(3):
            m["ck%d" % g] = f(caches[g][:, :, :, c, :]).reshape(32, caches[g].shape[1], 256)
        m["s0"] = f(st[:, 2 * c:2 * c + 2])
        m.update(_consts(c))
        in_maps.append(m)
    names = set(nc._k_in_names)
    in_maps = [{k: v for k, v in m.items() if k in names} for m in in_maps]
    res = run_bass_kernel_spmd(nc, in_maps, core_ids=list(range(NC_)), **({"trace": True} if os.environ.get("KTRACE") else {}))
    R = res.results
    if os.environ.get("KTRACE"):
        print("EXEC_TIME_NS", res.exec_time_ns)
    if DBG:
        _NC_CACHE["dbg"] = R
    y_p = np.zeros((1, 8192, D), np.float32); y_s = np.zeros((32, 8, D), np.float32)
    kvp = [np.zeros((1, 1, W, 2, 8, 128), np.float32) for W in (128, 512, 2048)]
    kvs = [np.zeros((1, 32, 8, 2, 8, 128), np.float32) for _ in range(3)]
    hg_p = np.zeros((1, 1, 16, 128, 128), np.float32); hg_s = np.zeros((1, 32, 16, 128, 128), np.float32)
    for c in range(NC_):
        r = R[c]
        y_p[0, 1024 * c:1024 * (c + 1)] = r["y"][:1024]
        y_s[4 * c:4 * c + 4] = r["y"][1024:].reshape(4, 8, D)
        off = 0
        for g, W in enumerate((128, 512, 2048)):
            kvp[g][0, 0, :, :, c, :] = r["kvp"][off:off + W].reshape(W, 2, 128)
            off += W
            kvs[g][0, :, :, :, c, :] = r["kvs"][g].reshape(32, 8, 2, 128)
        hg_p[0, 0, 2 * c:2 * c + 2] = r["hgp"]
        hg_s[0, :, 2 * c:2 * c + 2] = r["hgs"]
    return (y_p, y_s, kvp[0], kvp[1], kvp[2], hg_p, kvs[0], kvs[1], kvs[2], hg_s)
```
